# Optimizing a Trainium2 kernel written in Bass

```python
import jax, jax.numpy as jnp
from jax import lax
import numpy as np

D_MODEL = 4096
BATCH = 1
SEQ = 16384
DEPTH = 4

N_MIXERS = 2
FOX_HEADS = 32
FOX_HEAD_DIM = D_MODEL // FOX_HEADS
Q_BLOCK = 128
HGRN_EXPAND = 128
HGRN_HEADS = D_MODEL // HGRN_EXPAND
HGRN_DK = HGRN_EXPAND
HGRN_DV = D_MODEL // HGRN_HEADS
HGRN_KEY = HGRN_HEADS * HGRN_DK
CHUNK = 64
D_FF = 4 * D_MODEL
N_FOX = (DEPTH + 1) // 2
N_HGRN = DEPTH // 2
FOX_IN = 4 * D_MODEL + FOX_HEADS
HGRN_IN = 2 * HGRN_KEY + 2 * D_MODEL
NORM_EPS = 1e-6

kernel_name = "fox_hgrn2_interleaved_hybrid"


def rms_norm(x, gain):
    xf = x.astype(jnp.float32)
    y = xf * lax.rsqrt(jnp.mean(xf * xf, axis=-1, keepdims=True) + NORM_EPS)
    return (y * gain.astype(jnp.float32)).astype(x.dtype)


def forgetting_attention(q, k, v, log_f):
    B, S, H, Dh = q.shape
    nb = S // Q_BLOCK
    cum = jnp.cumsum(log_f, axis=1).transpose(0, 2, 1)
    kT = k.transpose(0, 2, 1, 3)
    vT = v.transpose(0, 2, 1, 3)
    q_blocks = q.reshape(B, nb, Q_BLOCK, H, Dh).transpose(1, 0, 3, 2, 4)
    c_blocks = cum.reshape(B, H, nb, Q_BLOCK).transpose(2, 0, 1, 3)
    starts = jnp.arange(nb, dtype=jnp.int32) * Q_BLOCK
    k_pos = jnp.arange(S, dtype=jnp.int32)
    scale = Dh ** -0.5

    def one_block(args):
        qb, cb, start = args
        s = jnp.einsum('bhqd,bhkd->bhqk', qb, kT, preferred_element_type=jnp.float32) * scale
        s = s + (cb[..., :, None] - cum[:, :, None, :])
        q_pos = start + jnp.arange(Q_BLOCK, dtype=jnp.int32)
        causal = k_pos[None, :] <= q_pos[:, None]
        p = jax.nn.softmax(jnp.where(causal, s, -jnp.inf), axis=-1)
        return jnp.einsum('bhqk,bhkd->bhqd', p.astype(vT.dtype), vT)

    o = lax.map(one_block, (q_blocks, c_blocks, starts))
    return o.transpose(1, 0, 3, 2, 4).reshape(B, S, H, Dh)


def hgrn2_chunked_scan(q, k, v, log_f):
    B, S, H, Dk = q.shape
    Dv = v.shape[-1]
    nc = S // CHUNK

    def to_chunks(t):
        return t.reshape(B, nc, CHUNK, H, t.shape[-1]).transpose(1, 0, 3, 2, 4)

    causal = jnp.tril(jnp.ones((CHUNK, CHUNK), dtype=bool))[:, :, None]

    def step(state, inp):
        qc, kc, vc, gc = inp
        qf, kf, vf = qc.astype(jnp.float32), kc.astype(jnp.float32), vc.astype(jnp.float32)
        b = jnp.cumsum(gc, axis=2)
        rel = b[:, :, :, None, :] - b[:, :, None, :, :]
        decay = jnp.exp(jnp.where(causal, rel, -jnp.inf))
        scores = jnp.einsum('bhtk,bhsk,bhtsk->bhts', qf, kf, decay)
        o = (jnp.einsum('bhts,bhsv->bhtv', scores, vf)
             + jnp.einsum('bhtk,bhkv->bhtv', qf * jnp.exp(b), state))
        b_last = b[:, :, -1:, :]
        state = (jnp.exp(b_last)[:, :, 0, :, None] * state
                 + jnp.einsum('bhsk,bhsv->bhkv', kf * jnp.exp(b_last - b), vf))
        return state, o

    state0 = jnp.zeros((B, H, Dk, Dv), jnp.float32)
    _, o = lax.scan(step, state0, (to_chunks(q), to_chunks(k), to_chunks(v), to_chunks(log_f)))
    return o.transpose(1, 0, 3, 2, 4).reshape(B, S, H, Dv)


def fox_mixer(h, w_in, w_out, q_gain, k_gain, fgate_bias):
    B, S, _ = h.shape
    proj = h @ w_in
    q, k, v, gate, fz = jnp.split(proj, [D_MODEL, 2 * D_MODEL, 3 * D_MODEL, 4 * D_MODEL], axis=-1)
    shp = (B, S, FOX_HEADS, FOX_HEAD_DIM)
    q = rms_norm(q.reshape(shp), q_gain)
    k = rms_norm(k.reshape(shp), k_gain)
    v = v.reshape(shp)
    log_f = jax.nn.log_sigmoid(fz.astype(jnp.float32) + fgate_bias.astype(jnp.float32))
    o = forgetting_attention(q, k, v, log_f)
    o = o.reshape(B, S, D_MODEL) * jax.nn.sigmoid(gate)
    return o @ w_out


def hgrn2_mixer(h, w_in, w_out, out_gain, lower_bound):
    B, S, _ = h.shape
    proj = h @ w_in
    q, fz, i_in, gate = jnp.split(proj, [HGRN_KEY, 2 * HGRN_KEY, 2 * HGRN_KEY + D_MODEL], axis=-1)
    kshp = (B, S, HGRN_HEADS, HGRN_DK)
    q = jax.nn.silu(q).reshape(kshp)
    fgate = lower_bound + (1.0 - lower_bound) * jax.nn.sigmoid(fz.astype(jnp.float32))
    log_f = jnp.log(fgate).reshape(kshp)
    k = (1.0 - fgate).astype(h.dtype).reshape(kshp)
    v = i_in.reshape(B, S, HGRN_HEADS, HGRN_DV)
    o = hgrn2_chunked_scan(q, k, v, log_f)
    o = rms_norm(o.astype(h.dtype), out_gain).reshape(B, S, D_MODEL) * jax.nn.silu(gate)
    return o @ w_out


def sq_relu_mlp(h, w_up, w_down):
    return jnp.square(jax.nn.relu(h @ w_up)) @ w_down


def setup_inputs(seed: int = 0) -> dict:
    key = jax.random.key(seed)
    ks = jax.random.split(key, 16)
    f32 = jnp.float32

    def normal(k, shape, scale):
        return jax.random.normal(k, shape, f32) * scale

    def gain(k, shape):
        return 1.0 + 0.02 * jax.random.normal(k, shape, f32)

    return {
        "x": normal(ks[0], (BATCH, SEQ, D_MODEL), 1.0),
        "fox_w_in": normal(ks[1], (N_FOX, D_MODEL, FOX_IN), D_MODEL ** -0.5),
        "fox_w_out": normal(ks[2], (N_FOX, D_MODEL, D_MODEL), D_MODEL ** -0.5),
        "fox_q_gain": gain(ks[3], (N_FOX, FOX_HEAD_DIM)),
        "fox_k_gain": gain(ks[4], (N_FOX, FOX_HEAD_DIM)),
        "fox_fgate_bias": jax.random.uniform(ks[5], (N_FOX, FOX_HEADS), f32, 1.0, 4.0),
        "hgrn_w_in": normal(ks[6], (N_HGRN, D_MODEL, HGRN_IN), D_MODEL ** -0.5),
        "hgrn_w_out": normal(ks[7], (N_HGRN, D_MODEL, D_MODEL), D_MODEL ** -0.5),
        "hgrn_out_gain": gain(ks[8], (N_HGRN, HGRN_DV)),
        "hgrn_lb_logits": normal(ks[9], (DEPTH, HGRN_KEY), 0.1),
        "mixer_norm_gain": gain(ks[10], (DEPTH, D_MODEL)),
        "mlp_norm_gain": gain(ks[11], (DEPTH, D_MODEL)),
        "mlp_w_up": normal(ks[12], (DEPTH, D_MODEL, D_FF), D_MODEL ** -0.5),
        "mlp_w_down": normal(ks[13], (DEPTH, D_FF, D_MODEL), D_FF ** -0.5),
        "final_norm_gain": gain(ks[14], (D_MODEL,)),
    }


def reference(x, fox_w_in, fox_w_out, fox_q_gain, fox_k_gain, fox_fgate_bias,
              hgrn_w_in, hgrn_w_out, hgrn_out_gain, hgrn_lb_logits,
              mixer_norm_gain, mlp_norm_gain, mlp_w_up, mlp_w_down, final_norm_gain):
    lb_p = jax.nn.softmax(hgrn_lb_logits.astype(jnp.float32), axis=0)
    lower_bounds = jnp.cumsum(lb_p, axis=0) - lb_p[0]
    h = x
    for i in range(DEPTH):
        j = i // N_MIXERS
        n = rms_norm(h, mixer_norm_gain[i])
        if i % N_MIXERS == 0:
            mix = fox_mixer(n, fox_w_in[j], fox_w_out[j], fox_q_gain[j], fox_k_gain[j], fox_fgate_bias[j])
        else:
            mix = hgrn2_mixer(n, hgrn_w_in[j], hgrn_w_out[j], hgrn_out_gain[j], lower_bounds[i])
        h = h + mix.astype(h.dtype)
        h = h + sq_relu_mlp(rms_norm(h, mlp_norm_gain[i]), mlp_w_up[i], mlp_w_down[i]).astype(h.dtype)
    return rms_norm(h, final_norm_gain)
```

```python
import os
import numpy as np
import concourse.bass as bass
import concourse.mybir as mybir
from concourse.bass_utils import run_bass_kernel_spmd

F32 = mybir.dt.float32
BF16 = mybir.dt.bfloat16
AF = mybir.ActivationFunctionType
ALU = mybir.AluOpType
EPS = 1e-6
KCUT = int(os.environ.get('KCUT', 99))
KFOX = int(os.environ.get('KFOX', 99))


class Cfg:
    def __init__(s, D=4096, S=16384, DEPTH=4, NCORES=8, stop=None):
        s.D = D; s.S = S; s.DEPTH = DEPTH; s.NC = NCORES
        s.H = D // 128; s.HPC = s.H // NCORES; s.TC = S // NCORES; s.KC = D // 128
        s.DFF = 4 * D; s.NFOX = (DEPTH + 1) // 2; s.NHG = DEPTH // 2
        s.TT = 512; s.NB = S // 128; s.NTT = S // 512; s.NTL = s.TC // 512
        s.stop = stop
        o = 0
        def take(n):
            nonlocal o
            r = o; o += n; return r
        s.o_ident = take(128); s.o_triu = take(128); s.o_ones = take(128); s.o_sel = take(128)
        s.o_rmask = take(512)
        s.o_g1 = take(DEPTH * s.KC); s.o_g2 = take(DEPTH * s.KC); s.o_gf = take(s.KC)
        s.o_qg = take(s.NFOX); s.o_kg = take(s.NFOX); s.o_fb = take(s.NFOX * s.HPC)
        s.o_hg = take(max(s.NHG, 1)); s.o_lbl = take(DEPTH * s.HPC)
        s.NSM = o


class Src:
    def __init__(s, nc, name):
        s.sem = nc.semaphore(name).__enter__(); s.n = 0


class Buf:
    def __init__(s, ap=None):
        s.w = None; s.r = {}; s.dsrc = None; s.ap = ap


class EngW:
    def __init__(s, nc, e, name):
        s.e = e; s.src = Src(nc, "c_" + name); s.seen = {}

    def wait(s, ev):
        if ev is None:
            return
        src, val = ev
        if s.seen.get(id(src), 0) >= val:
            return
        s.e.wait_ge(src.sem, val); s.seen[id(src)] = val

    def done(s, ins):
        ins.then_inc(s.src.sem, 1); s.src.n += 1
        return (s.src, s.src.n)


class KB:
    def __init__(s, cfg):
        s.cfg = cfg
        s.nc = nc = bass.Bass("TRN2", target_bir_lowering=False)
        s.E = {n: EngW(nc, getattr(nc, n), n) for n in ("tensor", "vector", "scalar", "gpsimd", "sync")}
        s.srcs = [e.src for e in s.E.values()]
        s.dpool = []; s.dpi = 0
        s.ccsrc = Src(nc, "cc"); s.srcs.append(s.ccsrc)

    def _pre(s, E, reads, writes):
        for b in reads:
            E.wait(b.w)
        for b in writes:
            E.wait(b.w)
            for ev in b.r.values():
                E.wait(ev)

    def _post(s, ev, reads, writes):
        for b in reads:
            b.r[id(ev[0])] = ev
        for b in writes:
            b.w = ev; b.r = {}

    def op(s, eng, fn, reads=(), writes=()):
        E = s.E[eng]
        s._pre(E, reads, writes)
        ins = fn(E.e)
        ev = E.done(ins)
        s._post(ev, reads, writes)

    def mm(s, fns, reads=(), writes=()):
        E = s.E["tensor"]
        s._pre(E, reads, writes)
        ins = None
        for f in fns:
            ins = f(E.e)
        ev = E.done(ins)
        s._post(ev, reads, writes)

    def _dsrc(s, b):
        if b.dsrc is None:
            if s.dpi >= len(s.dpool):
                src = Src(s.nc, "d%d" % len(s.dpool)); s.dpool.append(src); s.srcs.append(src)
            b.dsrc = s.dpool[s.dpi]; s.dpi += 1
        return b.dsrc

    def dma(s, q, pairs, reads=(), writes=(), key=None):
        E = s.E[q]
        s._pre(E, reads, writes)
        src = s._dsrc(key if key is not None else (writes[0] if writes else reads[0]))
        for (o, i) in pairs:
            E.e.dma_start(out=o, in_=i).then_inc(src.sem, 16); src.n += 16
        s._post((src, src.n), reads, writes)

    def barrier(s):
        for E in s.E.values():
            for src in s.srcs:
                if src.n > 0:
                    E.wait((src, src.n))

    def allgather_rows(s, in_t, out_t, R, C, esz=2):
        g = s.E["gpsimd"]
        NC = s.cfg.NC
        if NC == 1:
            g.e.dma_start(out=out_t[:, :], in_=in_t[:, :]).then_inc(s.ccsrc.sem, 16)
            s.ccsrc.n += 16
            g.wait((s.ccsrc, s.ccsrc.n))
            return
        cr = chunk_rows(R, C, esz)
        for k in range(R // cr):
            g.e.collective_compute("AllGather", ALU.bypass, replica_groups=[list(range(NC))],
                                   ins=[in_t[k * cr:(k + 1) * cr, :]],
                                   outs=[out_t[k * NC * cr:(k + 1) * NC * cr, :]]).then_inc(s.ccsrc.sem, 1)
            s.ccsrc.n += 1
            g.wait((s.ccsrc, s.ccsrc.n))


def chunk_rows(R, C, esz=2):
    cr = min(R, max(1, (512 * 1024) // (C * esz)))
    while R % cr:
        cr -= 1
    return cr


def build(cfg):
    kb = KB(cfg); nc = kb.nc
    D, S, TC, KC, HPC, NC, DFF, TT = cfg.D, cfg.S, cfg.TC, cfg.KC, cfg.HPC, cfg.NC, cfg.DFF, cfg.TT
    NB, NTT, NTL, DEPTH = cfg.NB, cfg.NTT, cfg.NTL, cfg.DEPTH
    HD = HPC * 128
    scl = 128.0 ** -0.5

    def din(name, shape, dt=F32):
        return nc.dram_tensor(name, shape, dt, kind="ExternalInput").ap()

    def dint(name, shape, dt, shared=False):
        if shared and NC > 1:
            return nc.dram_tensor(name, shape, dt, kind="Internal", addr_space="Shared").ap()
        return nc.dram_tensor(name, shape, dt, kind="Internal").ap()

    xT = din("xT", [D, TC])
    small = din("small", [128, cfg.NSM])
    fox_w = din("fox_w", [cfg.NFOX, D, HPC * 514])
    hg_w = din("hg_w", [max(cfg.NHG, 1), D, HPC * 512])
    fox_wo = din("fox_wo", [cfg.NFOX, D // NC, D])
    hg_wo = din("hg_wo", [max(cfg.NHG, 1), D // NC, D])
    w_up = din("w_up", [DEPTH, D // NC, DFF])
    w_dn = din("w_dn", [DEPTH, DFF // NC, D])
    yT = nc.dram_tensor("yT", [D, TC], F32, kind="ExternalOutput").ap()

    h_loc = dint("h_loc", [D, TC], F32)
    n_loc = dint("n_loc", [D, TC], BF16)
    n_all = dint("n_all", [NC * D, TC], BF16)
    og_loc = dint("og_loc", [NTT * HD, TT], BF16)
    og_all = dint("og_all", [NTT * D, TT], BF16)
    og_own = dint("og_own", [NTL * D, TT], BF16)
    wo_sh = [dint("wo_sh%d" % l, [D // NC, D], BF16) for l in range(DEPTH)]
    up_sh = [dint("up_sh%d" % l, [D // NC, DFF], BF16) for l in range(DEPTH)]
    dn_sh = [dint("dn_sh%d" % l, [DFF // NC, D], BF16) for l in range(DEPTH)]
    wo_all = [dint("wo_all%d" % l, [D, D], BF16) for l in range(DEPTH)]
    up_all = [dint("up_all%d" % l, [D, DFF], BF16) for l in range(DEPTH)]
    dn_all = [dint("dn_all%d" % l, [DFF, D], BF16) for l in range(DEPTH)]

    def sb(name, shape, dt):
        return nc.sbuf_tensor(name, shape, dt).__enter__()

    def ps(name, shape, dt=F32):
        return nc.psum_tensor(name, shape, dt).__enter__()

    sm = sb("sm", [128, cfg.NSM], F32); smB = Buf()
    ident = sm[:, cfg.o_ident:cfg.o_ident + 128]
    triu = sm[:, cfg.o_triu:cfg.o_triu + 128]
    onesf = sm[:, cfg.o_ones:cfg.o_ones + 128]
    sel63 = sm[:, cfg.o_sel:cfg.o_sel + 128]
    rmask = sm[:, cfg.o_rmask:cfg.o_rmask + 512]
    cb = sb("cb", [128, 256], BF16); cbB = Buf()
    ones_bf = cb[:, 0:128]; mask_bf = cb[:, 128:256]
    negb = sb("negb", [128, cfg.NFOX * HPC], F32)
    lbv = sb("lbv", [128, DEPTH * HPC], F32)
    omlb = sb("omlb", [128, DEPTH * HPC], F32)
    lbtmp = sb("lbtmp", [128, DEPTH * HPC + 2 * HPC], F32)
    AA = sb("AA", [128, 2 * KC * TT], BF16)
    A0 = AA[:, 0:KC * TT].rearrange("p (kc t) -> p kc t", t=TT)
    A1 = AA[:, KC * TT:2 * KC * TT].rearrange("p (kc t) -> p kc t", t=TT)
    sb_h = AA[:].bitcast(F32).rearrange("p (kc t) -> p kc t", t=TT)
    W0f = sb("W0", [128, max(KC * 516, NB * 130, 1024)], BF16); W1 = sb("W1", [128, KC, 516], BF16)
    W0 = W0f[:, 0:KC * 516].rearrange("p (kc n) -> p kc n", n=516)
    V0 = W0f
    U0 = sb("U0", [128, max(S, KC * TT)], BF16)
    NF = 10
    Ft = [sb("F%d" % i, [128, 512], F32) for i in range(NF)]
    Bt = [sb("Bt%d" % i, [128, 512], BF16) for i in range(10)]
    Lk = sb("Lk", [128, max(NB, 4)], F32)
    biasb = [sb("bias%d" % i, [128, max(NB, 4)], F32) for i in range(2)]
    sm8 = [sb("s8_%d" % i, [128, 16], F32) for i in range(6)]
    Sst = sb("Sst", [128, 128], F32); Sbf = sb("Sbf", [128, 128], BF16)
    G = [ps("G%d" % i, [128, 512]) for i in range(2)]
    SP = [ps("SP%d" % i, [128, 512]) for i in range(2)]
    OP = [ps("OP%d" % i, [128, 512]) for i in range(2)]
    X0 = ps("X0", [128, 512]); T0 = ps("T0", [128, 512])

    kb.dma("sync", [(sm[:], small[:, :])], writes=[smB])
    kb.op("vector", lambda e: e.tensor_copy(out=ones_bf, in_=onesf), reads=[smB], writes=[cbB])
    kb.op("vector", lambda e: e.tensor_copy(out=mask_bf, in_=triu), reads=[smB], writes=[cbB])
    misc = Buf()
    if cfg.NFOX:
        kb.op("vector", lambda e: e.tensor_scalar(out=negb[:], in0=sm[:, cfg.o_fb:cfg.o_fb + cfg.NFOX * HPC],
                                                  scalar1=-1.0, scalar2=None, op0=ALU.mult),
              reads=[smB], writes=[misc])
    nl = DEPTH * HPC
    lbl = sm[:, cfg.o_lbl:cfg.o_lbl + nl]
    kb.op("scalar", lambda e: e.activation(out=lbtmp[:, 0:nl], in_=lbl, func=AF.Exp), reads=[smB], writes=[misc])
    tot = lbtmp[:, nl:nl + HPC]; rt = lbtmp[:, nl + HPC:nl + 2 * HPC]
    kb.op("vector", lambda e: e.tensor_copy(out=tot, in_=lbtmp[:, 0:HPC]), reads=[misc], writes=[misc])
    for l in range(1, DEPTH):
        kb.op("vector", lambda e, l=l: e.tensor_tensor(out=tot, in0=tot, in1=lbtmp[:, l * HPC:(l + 1) * HPC], op=ALU.add),
              reads=[misc], writes=[misc])
    kb.op("vector", lambda e: e.reciprocal(out=rt, in_=tot), reads=[misc], writes=[misc])
    kb.op("vector", lambda e: e.memset(lbv[:, 0:HPC], 0.0), writes=[misc])
    for l in range(1, DEPTH):
        kb.op("vector", lambda e, l=l: e.tensor_tensor(out=lbtmp[:, l * HPC:(l + 1) * HPC], in0=lbtmp[:, l * HPC:(l + 1) * HPC],
                                                       in1=rt, op=ALU.mult), reads=[misc], writes=[misc])
        kb.op("vector", lambda e, l=l: e.tensor_tensor(out=lbv[:, l * HPC:(l + 1) * HPC], in0=lbv[:, (l - 1) * HPC:l * HPC],
                                                       in1=lbtmp[:, l * HPC:(l + 1) * HPC], op=ALU.add),
              reads=[misc], writes=[misc])
    kb.op("vector", lambda e: e.tensor_scalar(out=omlb[:], in0=lbv[:], scalar1=-1.0, scalar2=1.0, op0=ALU.mult, op1=ALU.add),
          reads=[misc], writes=[misc])

    hB = Buf()
    kb.dma("sync", [(h_loc[:, :], xT[:, :])], writes=[hB])
    wB = Buf()
    for l in range(DEPTH):
        src_wo = (fox_wo if l % 2 == 0 else hg_wo)[l // 2]
        kb.dma("gpsimd", [(wo_sh[l], src_wo)], writes=[wB])
        kb.dma("gpsimd", [(up_sh[l], w_up[l])], writes=[wB])
        kb.dma("gpsimd", [(dn_sh[l], w_dn[l])], writes=[wB])
    kb.barrier()
    for l in range(DEPTH):
        kb.allgather_rows(wo_sh[l], wo_all[l], D // NC, D)
        kb.allgather_rows(up_sh[l], up_all[l], D // NC, DFF)
        kb.allgather_rows(dn_sh[l], dn_all[l], DFF // NC, D)
    kb.barrier()

    KSTEP = 4

    def kcpairs(mk):
        return [mk(k0, min(k0 + KSTEP, KC)) for k0 in range(0, KC, KSTEP)]

    def load_act(dst, dstB, src_rows_ap):
        v = src_rows_ap.rearrange("(kc p) t -> p kc t", p=128)
        h = KC // 2 if KC >= 2 else KC
        pairs = [(dst[:, 0:h, :], v[:, 0:h, :])]
        if h < KC:
            pairs.append((dst[:, h:KC, :], v[:, h:KC, :]))
        kb.dma("sync", pairs, writes=[dstB])

    def rmsnorm_phase(gain_off, out_dram, out_is_f32):
        hT = sb_h
        hTB = Buf(); rsB = Buf()
        if KCUT < 1: return
        for tl in range(NTL):
            t0 = tl * TT
            v = h_loc[:, t0:t0 + TT].rearrange("(kc p) t -> p kc t", p=128)
            kb.dma("sync", kcpairs(lambda a, b: (hT[:, a:b, :], v[:, a:b, :])), writes=[hTB])
            sqB = [BB[0], BB[1]]
            if KCUT < 2: continue
            for kc in range(KC):
                b = sqB[kc % 2]
                kb.op("scalar", lambda e, kc=kc: e.activation(out=Bt[kc % 2][:], in_=hT[:, kc, :], func=AF.Square),
                      reads=[hTB], writes=[b])
                kb.mm([lambda e, kc=kc: e.matmul(X0[:], lhsT=ones_bf, rhs=Bt[kc % 2][:], start=(kc == 0), stop=(kc == KC - 1))],
                      reads=[b, cbB], writes=[X0B] if kc == 0 else [])
            X0B.w = (kb.E["tensor"].src, kb.E["tensor"].src.n); X0B.r = {}
            ssB = X0B
            if KCUT < 3: continue
            kb.op("scalar", lambda e: e.activation(out=Ft[0][:], in_=X0[:], func=AF.Sqrt, bias=EPS, scale=1.0 / D),
                  reads=[ssB], writes=[rsB])
            kb.op("vector", lambda e: e.reciprocal(out=Ft[0][:], in_=Ft[0][:]), reads=[rsB], writes=[rsB])
            if KCUT < 4: continue
            if out_is_f32:
                for kc in range(KC):
                    o = Ft[1 + kc % 4]
                    ob = oBs[kc % 4]
                    kb.op("vector", lambda e, kc=kc, o=o: e.scalar_tensor_tensor(
                        out=o[:], in0=hT[:, kc, :], scalar=sm[:, gain_off + kc:gain_off + kc + 1], op0=ALU.mult,
                        in1=Ft[0][:], op1=ALU.mult), reads=[hTB, rsB, smB], writes=[ob])
                    kb.dma("sync", [(out_dram[kc * 128:(kc + 1) * 128, t0:t0 + TT], o[:])], reads=[ob], key=ob)
            else:
                for kc in range(KC):
                    kb.op("vector", lambda e, kc=kc: e.scalar_tensor_tensor(
                        out=W0[:, kc, 0:TT], in0=hT[:, kc, :], scalar=sm[:, gain_off + kc:gain_off + kc + 1], op0=ALU.mult,
                        in1=Ft[0][:], op1=ALU.mult), reads=[hTB, rsB, smB], writes=[w0B])
                ov = out_dram[:, t0:t0 + TT].rearrange("(kc p) t -> p kc t", p=128)
                kb.dma("sync", kcpairs(lambda a, b: (ov[:, a:b, :], W0[:, a:b, 0:TT])), reads=[w0B], key=w0B)

    oBs = [Buf() for _ in range(4)]
    w0B = Buf(); w1B = Buf(); a0B = Buf(); a1B = Buf(); u0B = Buf(); v0B = Buf()
    GB = [Buf(), Buf()]; SPB = [Buf(), Buf()]; OPB = [Buf(), Buf()]; X0B = Buf(); T0B = Buf()
    FB = [Buf() for _ in range(NF)]; BB = [Buf() for _ in range(10)]
    LkB = Buf(); biasB = [Buf(), Buf()]; s8B = [Buf() for _ in range(6)]
    SstB = Buf(); SbfB = Buf()
    hdB = [Buf() for _ in range(KC)]
    pid = nc.partition_id([mybir.EngineType.SP])

    def gemm_cols(act, actB, wt, wtB, c0, ncols, gi):
        kb.mm([lambda e, kc=kc: e.matmul(G[gi][0:ncols, :], lhsT=wt[:, kc, c0:c0 + ncols], rhs=act[:, kc, :],
                                         start=(kc == 0), stop=(kc == KC - 1)) for kc in range(KC)],
              reads=[actB, wtB], writes=[GB[gi]])

    def fox_phase(j):
        kT = U0
        Vp = V0
        for hh in range(HPC):
            wv = fox_w[j][:, hh * 514:(hh + 1) * 514].rearrange("(kc p) n -> p kc n", p=128)
            step = min(KSTEP, KC)
            kb.dma("gpsimd", [(W1[:, k0:k0 + step, 0:514], wv[:, k0:k0 + step, :]) for k0 in range(0, KC, step)], writes=[w1B])
            kb.op("vector", lambda e: e.memset(Vp[:, 0:NB * 130], 1.0), writes=[v0B])
            kb.op("vector", lambda e: e.memset(sm8[0][:, 0:1], 0.0), writes=[s8B[0]])
            gi = 0
            for tt in range(NTT):
                act, actB = (A0, a0B) if tt % 2 == 0 else (A1, a1B)
                r = (tt * TT) // TC; toff = (tt * TT) % TC
                if KFOX < 2: continue
                nv = n_all.rearrange("(kc r p) t -> p kc r t", r=NC, p=128)[:, :, r, toff:toff + TT]
                kb.dma("sync", kcpairs(lambda a, b: (act[:, a:b, :], nv[:, a:b, :])), writes=[actB])
                for which in range(3):
                    gemm_cols(act, actB, W1, w1B, which * 128, 128, gi)
                    if which < 2:
                        qf, qfB = Ft[0 + which], FB[0 + which]
                        kb.op("scalar", lambda e, gi=gi, qf=qf: e.activation(out=qf[:], in_=G[gi][:], func=AF.Copy),
                              reads=[GB[gi]], writes=[qfB])
                        kb.op("scalar", lambda e, gi=gi, which=which: e.activation(out=Bt[which][:], in_=G[gi][:], func=AF.Square),
                              reads=[GB[gi]], writes=[BB[which]])
                        kb.mm([lambda e, which=which: e.matmul(X0[:], lhsT=ones_bf, rhs=Bt[which][:], start=True, stop=True)],
                              reads=[BB[which], cbB], writes=[X0B])
                        rs, rsB = Ft[2 + which], FB[2 + which]
                        kb.op("scalar", lambda e, rs=rs: e.activation(out=rs[:], in_=X0[:], func=AF.Sqrt, bias=EPS, scale=1.0 / 128),
                              reads=[X0B], writes=[rsB])
                        kb.op("vector", lambda e, rs=rs: e.reciprocal(out=rs[:], in_=rs[:]), reads=[rsB], writes=[rsB])
                        if which == 0:
                            dst = Bt[2][:]; dB = BB[2]
                            goff = cfg.o_qg + j
                        else:
                            dst = kT[:, tt * TT:(tt + 1) * TT]; dB = u0B
                            goff = cfg.o_kg + j
                        kb.op("vector", lambda e, qf=qf, rs=rs, dst=dst, goff=goff: e.scalar_tensor_tensor(
                            out=dst, in0=qf[:], scalar=sm[:, goff:goff + 1], op0=ALU.mult, in1=rs[:], op1=ALU.mult),
                            reads=[qfB, rsB, smB], writes=[dB])
                    else:
                        kb.op("scalar", lambda e, gi=gi: e.activation(out=Bt[3][:], in_=G[gi][:], func=AF.Sigmoid),
                              reads=[GB[gi]], writes=[BB[3]])
                    gi ^= 1
                if KFOX < 3: continue
                fzt, fztB = sm8[1], s8B[1]
                for st in range(4):
                    kb.mm([lambda e, kc=kc, st=st: e.matmul(G[gi][:, 0:130], lhsT=act[:, kc, st * 128:(st + 1) * 128],
                                                             rhs=W1[:, kc, 384:514], start=(kc == 0), stop=(kc == KC - 1))
                           for kc in range(KC)], reads=[actB, w1B], writes=[GB[gi]])
                    blk = tt * 4 + st
                    kb.op("vector", lambda e, gi=gi, blk=blk: e.tensor_copy(out=Vp[:, blk * 130:blk * 130 + 128], in_=G[gi][:, 0:128]),
                          reads=[GB[gi]], writes=[v0B])
                    kb.op("vector", lambda e, gi=gi, st=st: e.tensor_copy(out=fzt[:, st:st + 1], in_=G[gi][:, 128:129]),
                          reads=[GB[gi]], writes=[fztB])
                    gi ^= 1
                if KFOX < 4: continue
                lt, ltB = sm8[2], s8B[2]
                kb.op("scalar", lambda e: e.activation(out=lt[:, 0:4], in_=fzt[:, 0:4], func=AF.Exp,
                                                       bias=negb[:, j * HPC + hh:j * HPC + hh + 1], scale=-1.0),
                      reads=[fztB, misc], writes=[ltB])
                kb.op("scalar", lambda e: e.activation(out=lt[:, 0:4], in_=lt[:, 0:4], func=AF.Ln, bias=1.0, scale=1.0),
                      reads=[ltB], writes=[ltB])
                kb.mm([lambda e: e.matmul(X0[:, 0:4], lhsT=triu, rhs=lt[:, 0:4], start=True, stop=True),
                       lambda e: e.matmul(X0[:, 4:8], lhsT=onesf, rhs=lt[:, 0:4], start=True, stop=True)],
                      reads=[ltB, smB], writes=[X0B])
                Loff, LoffB = sm8[0], s8B[0]
                for st in range(4):
                    blk = tt * 4 + st
                    kb.op("vector", lambda e, st=st, blk=blk: e.tensor_scalar(out=Lk[:, blk:blk + 1], in0=X0[:, st:st + 1],
                                                                              scalar1=Loff[:, 0:1], scalar2=None, op0=ALU.add),
                          reads=[X0B, LoffB], writes=[LkB])
                    kb.op("vector", lambda e, st=st: e.tensor_tensor(out=Loff[:, 0:1], in0=Loff[:, 0:1], in1=X0[:, 4 + st:5 + st], op=ALU.add),
                          reads=[X0B], writes=[LoffB])
                kb.mm([lambda e: e.matmul(X0[:, 8:12], lhsT=sel63, rhs=Lk[:, tt * 4:tt * 4 + 4], start=True, stop=True)],
                      reads=[LkB, smB], writes=[X0B])
                Lref, LrefB = sm8[3], s8B[3]
                kb.op("vector", lambda e: e.tensor_copy(out=Lref[:, 0:4], in_=X0[:, 8:12]), reads=[X0B], writes=[LrefB])
                if KFOX < 5: continue
                qn, qnB = Bt[2], BB[2]
                sg, sgB = Bt[3], BB[3]
                og, ogB = Bt[4 + (tt % 2)], BB[4 + (tt % 2)]
                items = []
                for st in range(4):
                    i = tt * 4 + st
                    for g0 in range(0, i + 1, 4):
                        items.append((st, i, g0, min(g0 + 4, i + 1)))
                pbufs = [(Bt[6], BB[6]), (Bt[7], BB[7]), (Bt[8], BB[8])]

                def emit_S(n):
                    st, i, ja, jb = items[n]
                    sp = n % 2
                    kb.mm([lambda e, jj=jj: e.matmul(SP[sp][:, (jj - ja) * 128:(jj - ja + 1) * 128], lhsT=kT[:, jj * 128:(jj + 1) * 128],
                                                     rhs=qn[:, st * 128:(st + 1) * 128], start=True, stop=True) for jj in range(ja, jb)],
                          reads=[u0B, qnB], writes=[SPB[sp]])

                emit_S(0)
                for n in range(len(items)):
                    st, i, ja, jb = items[n]
                    sp = n % 2
                    pb, pbB = pbufs[n % 3]
                    bb, bbB = biasb[st % 2], biasB[st % 2]
                    if ja == 0:
                        kb.op("vector", lambda e, i=i, st=st, bb=bb: e.tensor_scalar(out=bb[:, 0:i + 1], in0=Lk[:, 0:i + 1], scalar1=Lref[:, st:st + 1],
                                                                                     scalar2=None, op0=ALU.subtract),
                              reads=[LkB, LrefB], writes=[bbB])
                    if n + 1 < len(items):
                        emit_S(n + 1)
                    for jj in range(ja, jb):
                        kb.op("scalar", lambda e, jj=jj, ja=ja, sp=sp, pb=pb, bb=bb: e.activation(
                            out=pb[:, (jj - ja) * 128:(jj - ja + 1) * 128], in_=SP[sp][:, (jj - ja) * 128:(jj - ja + 1) * 128],
                            func=AF.Exp, bias=bb[:, jj:jj + 1], scale=scl), reads=[SPB[sp], bbB], writes=[pbB])
                    if jb == i + 1:
                        kb.op("vector", lambda e, pb=pb, c=(i - ja) * 128: e.tensor_tensor(out=pb[:, c:c + 128], in0=pb[:, c:c + 128], in1=mask_bf, op=ALU.mult),
                              reads=[cbB], writes=[pbB])
                    ob = st % 2
                    kb.mm([lambda e, jj=jj, ja=ja, pb=pb, ob=ob, i=i: e.matmul(OP[ob][:, 0:130], lhsT=pb[:, (jj - ja) * 128:(jj - ja + 1) * 128],
                                                                            rhs=Vp[:, jj * 130:jj * 130 + 130], start=(jj == 0), stop=(jj == i))
                           for jj in range(ja, jb)], reads=[pbB, v0B], writes=[OPB[ob]])
                    if jb == i + 1:
                        rv, rvB = sm8[4], s8B[4]
                        kb.op("vector", lambda e, ob=ob: e.reciprocal(out=rv[:, 0:1], in_=OP[ob][:, 128:129]), reads=[OPB[ob]], writes=[rvB])
                        on, onB = Ft[4], FB[4]
                        kb.op("vector", lambda e, ob=ob: e.tensor_scalar(out=on[:, 0:128], in0=OP[ob][:, 0:128], scalar1=rv[:, 0:1], scalar2=None, op0=ALU.mult),
                              reads=[OPB[ob], rvB], writes=[onB])
                        kb.mm([lambda e: e.transpose(out=T0[:, 0:128], in_=on[:, 0:128], identity=ident)], reads=[onB, smB], writes=[T0B])
                        kb.op("vector", lambda e, st=st: e.tensor_tensor(out=og[:, st * 128:(st + 1) * 128], in0=T0[:, 0:128],
                                                                         in1=sg[:, st * 128:(st + 1) * 128], op=ALU.mult),
                              reads=[T0B, sgB], writes=[ogB])
                kb.dma("sync", [(og_loc[tt * HD + hh * 128:tt * HD + (hh + 1) * 128, :], og[:])], reads=[ogB], key=ogB)

    def hgrn_phase(j, layer):
        Vc = V0
        for hh in range(HPC):
            wv = hg_w[j][:, hh * 512:(hh + 1) * 512].rearrange("(kc p) n -> p kc n", p=128)
            step = min(KSTEP, KC)
            kb.dma("gpsimd", [(W1[:, k0:k0 + step, 0:512], wv[:, k0:k0 + step, :]) for k0 in range(0, KC, step)], writes=[w1B])
            kb.op("vector", lambda e: e.memset(Sst[:], 0.0), writes=[SstB])
            kb.op("vector", lambda e: e.memset(Sbf[:], 0.0), writes=[SbfB])
            lbc = lbv[:, layer * HPC + hh:layer * HPC + hh + 1]
            omc = omlb[:, layer * HPC + hh:layer * HPC + hh + 1]
            gi = 0
            for tt in range(NTT):
                act, actB = (A0, a0B) if tt % 2 == 0 else (A1, a1B)
                r = (tt * TT) // TC; toff = (tt * TT) % TC
                nv = n_all.rearrange("(kc r p) t -> p kc r t", r=NC, p=128)[:, :, r, toff:toff + TT]
                kb.dma("sync", kcpairs(lambda a, b: (act[:, a:b, :], nv[:, a:b, :])), writes=[actB])
                qs, qsB = Ft[0], FB[0]
                fv, fvB = Ft[1], FB[1]
                gemm_cols(act, actB, W1, w1B, 0, 128, gi)
                kb.op("scalar", lambda e, gi=gi: e.activation(out=qs[:], in_=G[gi][:], func=AF.Silu), reads=[GB[gi]], writes=[qsB])
                gi ^= 1
                gemm_cols(act, actB, W1, w1B, 128, 128, gi)
                kb.op("scalar", lambda e, gi=gi: e.activation(out=fv[:], in_=G[gi][:], func=AF.Sigmoid), reads=[GB[gi]], writes=[fvB])
                gi ^= 1
                kb.op("vector", lambda e: e.tensor_scalar(out=fv[:], in0=fv[:], scalar1=omc, scalar2=lbc, op0=ALU.mult, op1=ALU.add),
                      reads=[misc], writes=[fvB])
                gemm_cols(act, actB, W1, w1B, 256, 128, gi)
                kb.op("scalar", lambda e, gi=gi: e.activation(out=Bt[3][:], in_=G[gi][:], func=AF.Silu), reads=[GB[gi]], writes=[BB[3]])
                gi ^= 1
                for c in range(8):
                    kb.mm([lambda e, kc=kc, c=c: e.matmul(G[gi][0:64, 0:128], lhsT=act[:, kc, c * 64:(c + 1) * 64], rhs=W1[:, kc, 384:512],
                                                           start=(kc == 0), stop=(kc == KC - 1)) for kc in range(KC)],
                          reads=[actB, w1B], writes=[GB[gi]])
                    kb.op("vector", lambda e, gi=gi, c=c: e.tensor_copy(out=Vc[0:64, c * 128:(c + 1) * 128], in_=G[gi][0:64, 0:128]),
                          reads=[GB[gi]], writes=[v0B])
                    gi ^= 1
                lg, lgB = Ft[2], FB[2]
                kk, kkB = Ft[3], FB[3]
                kb.op("scalar", lambda e: e.activation(out=lg[:], in_=fv[:], func=AF.Ln), reads=[fvB], writes=[lgB])
                kb.op("vector", lambda e: e.tensor_scalar(out=kk[:], in0=fv[:], scalar1=-1.0, scalar2=1.0, op0=ALU.mult, op1=ALU.add),
                      reads=[fvB], writes=[kkB])
                bq, bqB = Ft[4], FB[4]
                kb.op("vector", lambda e: e.tensor_tensor_scan(out=bq[:], data0=rmask, data1=lg[:], initial=0.0, op0=ALU.mult, op1=ALU.add),
                      reads=[lgB, smB], writes=[bqB])
                b3 = bq[:].rearrange("p (c t) -> p c t", t=64)
                bm = b3[:, :, 31:32].to_broadcast([128, 8, 64])
                bl = b3[:, :, 63:64].to_broadcast([128, 8, 64])
                d1, d1B = Ft[5], FB[5]
                d2, d2B = Ft[6], FB[6]
                kb.op("vector", lambda e: e.tensor_tensor(out=d1[:].rearrange("p (c t) -> p c t", t=64), in0=b3, in1=bm, op=ALU.subtract),
                      reads=[bqB], writes=[d1B])
                kb.op("vector", lambda e: e.tensor_tensor(out=d2[:].rearrange("p (c t) -> p c t", t=64), in0=bl, in1=b3, op=ALU.subtract),
                      reads=[bqB], writes=[d2B])
                dl, dlB = sm8[5], s8B[5]
                kb.op("scalar", lambda e: e.activation(out=dl[:, 0:8], in_=bq[:].rearrange("p (c t) -> p c t", t=64)[:, :, 63], func=AF.Exp),
                      reads=[bqB], writes=[dlB])
                e1, e1B = Ft[7], FB[7]
                kb.op("scalar", lambda e: e.activation(out=e1[:], in_=d1[:], func=AF.Exp), reads=[d1B], writes=[e1B])
                kb.op("vector", lambda e: e.tensor_tensor(out=Bt[0][:], in0=qs[:], in1=e1[:], op=ALU.mult), reads=[qsB, e1B], writes=[BB[0]])
                kb.op("scalar", lambda e: e.activation(out=e1[:], in_=d1[:], func=AF.Exp, scale=-1.0), reads=[d1B], writes=[e1B])
                kb.op("vector", lambda e: e.tensor_tensor(out=Bt[1][:], in0=kk[:], in1=e1[:], op=ALU.mult), reads=[kkB, e1B], writes=[BB[1]])
                kb.op("scalar", lambda e: e.activation(out=e1[:], in_=bq[:], func=AF.Exp), reads=[bqB], writes=[e1B])
                kb.op("vector", lambda e: e.tensor_tensor(out=Bt[2][:], in0=qs[:], in1=e1[:], op=ALU.mult), reads=[qsB, e1B], writes=[BB[2]])
                kb.op("scalar", lambda e: e.activation(out=e1[:], in_=d2[:], func=AF.Exp), reads=[d2B], writes=[e1B])
                kh, khB = Ft[8], FB[8]
                kb.op("vector", lambda e: e.tensor_tensor(out=kh[:], in0=kk[:], in1=e1[:], op=ALU.mult), reads=[kkB, e1B], writes=[khB])
                og, ogB = Bt[4 + (tt % 2)], BB[4 + (tt % 2)]
                sgs, sgB = Bt[3], BB[3]
                for c in range(8):
                    cs = slice(c * 64, (c + 1) * 64)
                    sp = c % 2
                    kb.mm([lambda e, cs=cs, sp=sp: e.matmul(SP[sp][0:64, 0:64], lhsT=Bt[1][:, cs], rhs=Bt[0][:, cs], start=True, stop=True)],
                          reads=[BB[0], BB[1]], writes=[SPB[sp]])
                    pt, ptB = Bt[6 + sp], BB[6 + sp]
                    kb.op("vector", lambda e, sp=sp, pt=pt: e.tensor_tensor(out=pt[0:64, 0:64], in0=SP[sp][0:64, 0:64], in1=triu[0:64, 0:64], op=ALU.mult),
                          reads=[SPB[sp], smB], writes=[ptB])
                    kb.mm([lambda e, cs=cs: e.transpose(out=T0[0:64, 0:128], in_=kh[:, cs], identity=ident)], reads=[khB, smB], writes=[T0B])
                    kt, ktB = Bt[8 + sp], BB[8 + sp]
                    kb.op("scalar", lambda e, kt=kt: e.activation(out=kt[0:64, 0:128], in_=T0[0:64, 0:128], func=AF.Copy), reads=[T0B], writes=[ktB])
                    ob = c % 2
                    kb.mm([lambda e, pt=pt, c=c, ob=ob: e.matmul(OP[ob][0:64, 0:128], lhsT=pt[0:64, 0:64], rhs=Vc[0:64, c * 128:(c + 1) * 128], start=True, stop=False),
                           lambda e, cs=cs, ob=ob: e.matmul(OP[ob][0:64, 0:128], lhsT=Bt[2][:, cs], rhs=Sbf[:], start=False, stop=True)],
                          reads=[ptB, v0B, BB[2], SbfB], writes=[OPB[ob]])
                    kb.mm([lambda e, kt=kt, c=c: e.matmul(X0[:, 0:128], lhsT=kt[0:64, 0:128], rhs=Vc[0:64, c * 128:(c + 1) * 128], start=True, stop=True)],
                          reads=[ktB, v0B], writes=[X0B])
                    kb.op("vector", lambda e, c=c: e.scalar_tensor_tensor(out=Sst[:], in0=Sst[:], scalar=dl[:, c:c + 1], op0=ALU.mult, in1=X0[:, 0:128], op1=ALU.add),
                          reads=[X0B, dlB], writes=[SstB])
                    kb.op("scalar", lambda e: e.activation(out=Sbf[:], in_=Sst[:], func=AF.Copy), reads=[SstB], writes=[SbfB])
                    ss, ssB_ = sm8[4], s8B[4]
                    junk, junkB = Ft[9], FB[9]
                    kb.op("scalar", lambda e, ob=ob: e.activation(out=junk[0:64, 0:128], in_=OP[ob][0:64, 0:128], func=AF.Square, accum_out=ss[0:64, 0:1]),
                          reads=[OPB[ob]], writes=[junkB, ssB_])
                    kb.op("scalar", lambda e: e.activation(out=ss[0:64, 0:1], in_=ss[0:64, 0:1], func=AF.Sqrt, bias=EPS, scale=1.0 / 128), writes=[ssB_])
                    kb.op("vector", lambda e: e.reciprocal(out=ss[0:64, 0:1], in_=ss[0:64, 0:1]), writes=[ssB_])
                    kb.op("vector", lambda e, ob=ob: e.tensor_scalar(out=junk[0:64, 0:128], in0=OP[ob][0:64, 0:128], scalar1=ss[0:64, 0:1], scalar2=None, op0=ALU.mult),
                          reads=[OPB[ob], ssB_], writes=[junkB])
                    kb.mm([lambda e: e.transpose(out=T0[:, 128:192], in_=junk[0:64, 0:128], identity=ident[0:64, 0:64])], reads=[junkB, smB], writes=[T0B])
                    kb.op("vector", lambda e, cs=cs: e.scalar_tensor_tensor(out=og[:, cs], in0=T0[:, 128:192], scalar=sm[:, cfg.o_hg + j:cfg.o_hg + j + 1], op0=ALU.mult,
                                                                           in1=sgs[:, cs], op1=ALU.mult), reads=[T0B, sgB, smB], writes=[ogB])
                kb.dma("sync", [(og_loc[tt * HD + hh * 128:tt * HD + (hh + 1) * 128, :], og[:])], reads=[ogB], key=ogB)

    def resid_gemm(act, actB, w_dram, t0):
        wv = w_dram.rearrange("(kc p) n -> p kc n", p=128)
        ngrp = D // 512
        wb = [(W0, w0B), (W1, w1B)]

        def loadw(g):
            wt, wtB = wb[g % 2]
            q = "sync" if g % 2 == 0 else "gpsimd"
            kb.dma(q, kcpairs(lambda a, b: (wt[:, a:b, 0:512], wv[:, a:b, g * 512:(g + 1) * 512])), writes=[wtB])
        loadw(0)
        gi = 0
        for g in range(ngrp):
            if g + 1 < ngrp:
                loadw(g + 1)
            wt, wtB = wb[g % 2]
            for sc in range(4):
                n0 = g * 512 + sc * 128
                hb, hbB = Ft[(g * 4 + sc) % 4], FB[(g * 4 + sc) % 4]
                hd = hdB[n0 // 128]
                kb.dma("sync", [(hb[:], h_loc[n0:n0 + 128, t0:t0 + TT])], reads=[hd], writes=[hbB], key=hbB)
                gemm_cols(act, actB, wt, wtB, sc * 128, 128, gi)
                kb.op("vector", lambda e, gi=gi, hb=hb: e.tensor_tensor(out=hb[:], in0=G[gi][:], in1=hb[:], op=ALU.add),
                      reads=[GB[gi]], writes=[hbB])
                kb.dma("sync", [(h_loc[n0:n0 + 128, t0:t0 + TT], hb[:])], reads=[hbB], writes=[hd], key=hbB)
                gi ^= 1

    def wout_phase(layer):
        for tl in range(NTL):
            act, actB = (A0, a0B) if tl % 2 == 0 else (A1, a1B)
            t0 = tl * TT
            ogv = og_own.rearrange("(j p) t -> p j t", p=128)
            kb.dma("sync", kcpairs(lambda a, b: (act[:, a:b, :], ogv[:, tl * KC + a:tl * KC + b, :])), writes=[actB])
            resid_gemm(act, actB, wo_all[layer], t0)

    def mlp_phase(layer):
        uT = U0[:, 0:KC * TT].rearrange("p (kc t) -> p kc t", t=TT)
        wb = [(W0, w0B), (W1, w1B)]
        for tl in range(NTL):
            act, actB = (A0, a0B) if tl % 2 == 0 else (A1, a1B)
            t0 = tl * TT
            load_act(act, actB, n_loc[:, t0:t0 + TT])
            for fg in range(DFF // D):
                wv = up_all[layer][:, fg * D:(fg + 1) * D].rearrange("(kc p) n -> p kc n", p=128)
                ngrp = D // 512

                def loadw(g):
                    wt, wtB = wb[g % 2]
                    q = "sync" if g % 2 == 0 else "gpsimd"
                    kb.dma(q, kcpairs(lambda a, b: (wt[:, a:b, 0:512], wv[:, a:b, g * 512:(g + 1) * 512])), writes=[wtB])
                loadw(0)
                gi = 0
                for g in range(ngrp):
                    if g + 1 < ngrp:
                        loadw(g + 1)
                    wt, wtB = wb[g % 2]
                    for sc in range(4):
                        fc = g * 4 + sc
                        gemm_cols(act, actB, wt, wtB, sc * 128, 128, gi)
                        rl, rlB = Ft[4 + fc % 2], FB[4 + fc % 2]
                        kb.op("scalar", lambda e, gi=gi, rl=rl: e.activation(out=rl[:], in_=G[gi][:], func=AF.Relu), reads=[GB[gi]], writes=[rlB])
                        kb.op("vector", lambda e, rl=rl, fc=fc: e.tensor_tensor(out=uT[:, fc, :], in0=rl[:], in1=rl[:], op=ALU.mult),
                              reads=[rlB], writes=[u0B])
                        gi ^= 1
                resid_gemm(uT, u0B, dn_all[layer][fg * D:(fg + 1) * D, :], t0)


    state = {"n": 0}

    def more():
        state["n"] += 1
        return cfg.stop is None or state["n"] <= cfg.stop

    done = False
    for layer in range(DEPTH):
        j = layer // 2
        if not more():
            done = True; break
        rmsnorm_phase(cfg.o_g1 + layer * KC, n_loc, False)
        kb.barrier()
        if os.environ.get("KSKIP") != "ag":
            if NC == 1:
                kb.allgather_rows(n_loc, n_all, D, TC)
            else:
                g = kb.E["gpsimd"]
                for kc in range(KC):
                    g.e.collective_compute("AllGather", ALU.bypass, replica_groups=[list(range(NC))],
                                           ins=[n_loc[kc * 128:(kc + 1) * 128, :]],
                                           outs=[n_all[kc * NC * 128:(kc + 1) * NC * 128, :]]).then_inc(kb.ccsrc.sem, 1)
                    kb.ccsrc.n += 1
                    g.wait((kb.ccsrc, kb.ccsrc.n))
        kb.barrier()
        if not more():
            done = True; break
        if layer % 2 == 0:
            fox_phase(j)
        else:
            hgrn_phase(j, layer)
        kb.barrier()
        if NC == 1:
            kb.allgather_rows(og_loc, og_all, NTT * HD, TT)
        else:
            g = kb.E["gpsimd"]
            for tq in range(NTT):
                g.e.collective_compute("AllGather", ALU.bypass, replica_groups=[list(range(NC))],
                                       ins=[og_loc[tq * HD:(tq + 1) * HD, :]],
                                       outs=[og_all[tq * D:(tq + 1) * D, :]]).then_inc(kb.ccsrc.sem, 1)
                kb.ccsrc.n += 1
                g.wait((kb.ccsrc, kb.ccsrc.n))
        kb.barrier()
        ogc = og_all.rearrange("(c r) t -> c r t", c=NC)
        ownB = Buf()
        for tl in range(NTL):
            kb.dma("sync", [(og_own[tl * D:(tl + 1) * D, :].unsqueeze(0), ogc[bass.ds(pid, 1), tl * D:(tl + 1) * D, :])], writes=[ownB])
        kb.barrier()
        if not more():
            done = True; break
        wout_phase(layer)
        kb.barrier()
        if not more():
            done = True; break
        rmsnorm_phase(cfg.o_g2 + layer * KC, n_loc, False)
        kb.barrier()
        mlp_phase(layer)
        kb.barrier()
    if done:
        if cfg.stop is not None and cfg.stop < 0:
            pass
        kb.dma("sync", [(yT[:, :], h_loc[:, :])], writes=[Buf()])
    else:
        rmsnorm_phase(cfg.o_gf, yT, True)
    kb.barrier()
    return kb


def host_inputs(cfg, x, fox_w_in, fox_w_out, fox_q_gain, fox_k_gain, fox_fgate_bias, hgrn_w_in, hgrn_w_out,
                hgrn_out_gain, hgrn_lb_logits, mixer_norm_gain, mlp_norm_gain, mlp_w_up, mlp_w_down, final_norm_gain):
    D, S, TC, KC, HPC, NC, DFF, DEPTH = cfg.D, cfg.S, cfg.TC, cfg.KC, cfg.HPC, cfg.NC, cfg.DFF, cfg.DEPTH
    f32 = np.float32
    x = np.asarray(x, f32); maps = []
    ident = np.eye(128, dtype=f32); triu = np.triu(np.ones((128, 128), f32)); ones = np.ones((128, 128), f32)
    sel = np.zeros((128, 128), f32); sel[63, :] = 1.0
    rmask = np.ones((128, 512), f32); rmask[:, ::64] = 0.0
    def fm(v):
        v = np.asarray(v, f32).reshape(-1, KC, 128)
        return np.ascontiguousarray(v.transpose(2, 0, 1)).reshape(128, -1)
    for c in range(NC):
        hs = [c * HPC + i for i in range(HPC)]
        sm = np.zeros((128, cfg.NSM), f32)
        sm[:, cfg.o_ident:cfg.o_ident + 128] = ident; sm[:, cfg.o_triu:cfg.o_triu + 128] = triu
        sm[:, cfg.o_ones:cfg.o_ones + 128] = ones; sm[:, cfg.o_sel:cfg.o_sel + 128] = sel
        sm[:, cfg.o_rmask:cfg.o_rmask + 512] = rmask
        sm[:, cfg.o_g1:cfg.o_g1 + DEPTH * KC] = fm(mixer_norm_gain)
        sm[:, cfg.o_g2:cfg.o_g2 + DEPTH * KC] = fm(mlp_norm_gain)
        sm[:, cfg.o_gf:cfg.o_gf + KC] = fm(np.asarray(final_norm_gain)[None])
        sm[:, cfg.o_qg:cfg.o_qg + cfg.NFOX] = np.asarray(fox_q_gain, f32).T
        sm[:, cfg.o_kg:cfg.o_kg + cfg.NFOX] = np.asarray(fox_k_gain, f32).T
        fb = np.asarray(fox_fgate_bias, f32)[:, hs]
        sm[:, cfg.o_fb:cfg.o_fb + cfg.NFOX * HPC] = np.broadcast_to(fb.reshape(1, -1), (128, cfg.NFOX * HPC))
        if cfg.NHG:
            sm[:, cfg.o_hg:cfg.o_hg + cfg.NHG] = np.asarray(hgrn_out_gain, f32).T
        lbl = np.asarray(hgrn_lb_logits, f32).reshape(DEPTH, cfg.H, 128)[:, hs, :]
        sm[:, cfg.o_lbl:cfg.o_lbl + DEPTH * HPC] = lbl.transpose(2, 0, 1).reshape(128, -1)
        m = {"small": sm}
        m["xT"] = np.ascontiguousarray(x[0, c * TC:(c + 1) * TC, :].T)
        fw = np.zeros((cfg.NFOX, D, HPC * 514), f32)
        for jj in range(cfg.NFOX):
            W = fox_w_in[jj]
            for i, h in enumerate(hs):
                o = i * 514
                fw[jj, :, o:o + 128] = W[:, h * 128:(h + 1) * 128]
                fw[jj, :, o + 128:o + 256] = W[:, D + h * 128:D + (h + 1) * 128]
                fw[jj, :, o + 256:o + 384] = W[:, 3 * D + h * 128:3 * D + (h + 1) * 128]
                fw[jj, :, o + 384:o + 512] = W[:, 2 * D + h * 128:2 * D + (h + 1) * 128]
                fw[jj, :, o + 512] = W[:, 4 * D + h]
        m["fox_w"] = fw
        hw = np.zeros((max(cfg.NHG, 1), D, HPC * 512), f32)
        for jj in range(cfg.NHG):
            W = hgrn_w_in[jj]
            for i, h in enumerate(hs):
                o = i * 512
                hw[jj, :, o:o + 128] = W[:, h * 128:(h + 1) * 128]
                hw[jj, :, o + 128:o + 256] = W[:, D + h * 128:D + (h + 1) * 128]
                hw[jj, :, o + 256:o + 384] = W[:, 3 * D + h * 128:3 * D + (h + 1) * 128]
                hw[jj, :, o + 384:o + 512] = W[:, 2 * D + h * 128:2 * D + (h + 1) * 128]
        m["hg_w"] = hw
        def shard(W, R, C):
            W = np.asarray(W, f32); cr = chunk_rows(R, C)
            return np.ascontiguousarray(W.reshape(W.shape[0], R // cr, NC, cr, C)[:, :, c].reshape(W.shape[0], R, C))
        m["fox_wo"] = shard(fox_w_out, D // NC, D)
        m["hg_wo"] = shard(hgrn_w_out, D // NC, D) if cfg.NHG else np.zeros((1, D // NC, D), f32)
        m["w_up"] = shard(mlp_w_up, D // NC, DFF)
        m["w_dn"] = shard(mlp_w_down, DFF // NC, D)
        maps.append(m)
    return maps


_CACHE = {}


def run(cfg, inputs):
    key = (cfg.D, cfg.S, cfg.DEPTH, cfg.NC, cfg.stop)
    if key not in _CACHE:
        _CACHE[key] = build(cfg)
    kb = _CACHE[key]
    maps = host_inputs(cfg, **inputs)
    res = run_bass_kernel_spmd(kb.nc, maps, core_ids=list(range(cfg.NC)))
    out = np.empty((1, cfg.S, cfg.D), np.float32)
    for c in range(cfg.NC):
        out[0, c * cfg.TC:(c + 1) * cfg.TC, :] = res.results[c]["yT"].T
    return out


def kernel(**inputs):
    cfg = Cfg()
    return run(cfg, inputs)
```
